# Optimizing a Trainium2 kernel written in Bass

```python
import math
import jax
import jax.numpy as jnp
from jax import lax
import numpy as np

D_MODEL = 4096
BATCH = 1
SEQ = 16384
DEPTH = 4

CHUNK = 64
WINDOW = 128
ATTN_BLOCK = 128
HEAD_DIM = 128
N_Q_HEADS = D_MODEL // 256
N_KV_HEADS = max(1, N_Q_HEADS // 8)
GQA_GROUP = N_Q_HEADS // N_KV_HEADS
ATTN_WIDTH = N_Q_HEADS * HEAD_DIM
KV_WIDTH = N_KV_HEADS * HEAD_DIM
ROT_DIM = HEAD_DIM // 4
ROPE_THETA = 500000.0
SSM_WIDTH = D_MODEL // 2
SSM_GROUP_CH = 16
SSM_GROUPS = SSM_WIDTH // SSM_GROUP_CH
SSM_STATE = 64
SSM_DT_MIN = 1e-3
SSM_DT_MAX = 1e-1
D_FF = D_MODEL
N_MEM = 256
XA_HEADS = 4
XA_HEAD_DIM = 128
XA_WIDTH = XA_HEADS * XA_HEAD_DIM
IN_COLS = ATTN_WIDTH + 2 * KV_WIDTH + SSM_WIDTH + 2 * D_MODEL
N_GAINS = 8
RMS_EPS = 1e-6
MASK_VALUE = -1e30

kernel_name = 'hybrid_swa_s5_macaron_stream_encoder'


def rmsnorm(x, gain):
    xf = x.astype(jnp.float32)
    y = xf * lax.rsqrt(jnp.mean(xf * xf, axis=-1, keepdims=True) + RMS_EPS)
    return (y * gain.astype(jnp.float32)).astype(x.dtype)


def swiglu(x, w_gu, w_down):
    gate, up = jnp.split(x @ w_gu, 2, axis=-1)
    return (jax.nn.silu(gate) * up) @ w_down


def rope_tables(positions):
    inv_freq = ROPE_THETA ** (-jnp.arange(0, ROT_DIM, 2, dtype=jnp.float32) / ROT_DIM)
    ang = positions.astype(jnp.float32)[..., None] * inv_freq
    return jnp.cos(ang)[:, :, None, :], jnp.sin(ang)[:, :, None, :]


def apply_partial_rope(t, cos, sin):
    half = ROT_DIM // 2
    tr = t[..., :ROT_DIM].astype(jnp.float32)
    t1, t2 = tr[..., :half], tr[..., half:]
    rot = jnp.concatenate([t1 * cos - t2 * sin, t2 * cos + t1 * sin], axis=-1)
    return jnp.concatenate([rot.astype(t.dtype), t[..., ROT_DIM:]], axis=-1)


def band_blocks(t):
    b, l = t.shape[:2]
    blk = t.reshape(b, l // ATTN_BLOCK, ATTN_BLOCK, *t.shape[2:])
    prev = jnp.concatenate([jnp.zeros_like(blk[:, :1]), blk[:, :-1]], axis=1)
    return jnp.concatenate([prev, blk], axis=2)


def sliding_window_gqa_with_sinks(q, k, v, sinks):
    b, l = q.shape[:2]
    nb = l // ATTN_BLOCK
    cpb = ATTN_BLOCK // CHUNK
    win_chunks = WINDOW // CHUNK
    qb = q.reshape(b, nb, ATTN_BLOCK, N_KV_HEADS, GQA_GROUP, HEAD_DIM)
    kb, vb = band_blocks(k), band_blocks(v)
    s = jnp.einsum('bnqkgd,bnskd->bnkgqs', qb, kb).astype(jnp.float32) * (HEAD_DIM ** -0.5)
    q_chunk = jnp.arange(ATTN_BLOCK) // CHUNK
    k_chunk = jnp.arange(2 * ATTN_BLOCK) // CHUNK
    band = (k_chunk[None, :] >= q_chunk[:, None] + cpb - win_chunks) & (k_chunk[None, :] <= q_chunk[:, None] + cpb)
    k_pos = jnp.arange(nb)[:, None] * ATTN_BLOCK - ATTN_BLOCK + jnp.arange(2 * ATTN_BLOCK)[None, :]
    mask = band[None] & (k_pos >= 0)[:, None, :]
    s = jnp.where(mask[None, :, None, None], s, MASK_VALUE)
    sink = sinks.astype(jnp.float32).reshape(N_KV_HEADS, GQA_GROUP)[None, None, :, :, None]
    m = jnp.maximum(jnp.max(s, axis=-1), sink)
    p = jnp.exp(s - m[..., None])
    probs = p / (jnp.sum(p, axis=-1) + jnp.exp(sink - m))[..., None]
    o = jnp.einsum('bnkgqs,bnskd->bnqkgd', probs.astype(v.dtype), vb)
    return o.reshape(b, l, ATTN_WIDTH)


def s5_glu_branch(u, lam_re, lam_im, log_dt, b_re, b_im, c_re, c_im, d_skip, w_glu):
    b, l = u.shape[:2]
    f32 = jnp.float32
    ug = u.astype(f32).reshape(b, l, SSM_GROUPS, SSM_GROUP_CH)
    lam = lax.complex(jnp.minimum(lam_re.astype(f32), -1e-4), lam_im.astype(f32))
    dt = jnp.exp(log_dt.astype(f32))[:, None]
    log_lam_bar = lam * dt
    lam_bar = jnp.exp(log_lam_bar)
    b_bar = ((lam_bar - 1.0) / lam)[:, :, None] * lax.complex(b_re.astype(f32), b_im.astype(f32))
    bu = jnp.einsum('gph,blgh->blgp', b_bar, ug.astype(jnp.complex64))
    steps = jnp.ones((1, l, 1, 1), f32)

    def combine(earlier, later):
        n_e, x_e = earlier
        n_l, x_l = later
        return n_e + n_l, jnp.exp(n_l * log_lam_bar) * x_e + x_l

    _, states = lax.associative_scan(combine, (steps, bu), axis=1)
    c = lax.complex(c_re.astype(f32), c_im.astype(f32))
    y = jnp.einsum('ghp,blgp->blgh', c, states).real + d_skip.astype(f32) * ug
    y = jax.nn.gelu(y.reshape(b, l, SSM_WIDTH)).astype(u.dtype)
    return y * jax.nn.sigmoid(y @ w_glu)


def memory_cross_attention(xn, memn, wq, wkv, wo):
    b, l = xn.shape[:2]
    q = (xn @ wq).reshape(b, l, XA_HEADS, XA_HEAD_DIM)
    k, v = jnp.split(memn @ wkv, 2, axis=-1)
    k = k.reshape(b, -1, XA_HEADS, XA_HEAD_DIM)
    v = v.reshape(b, -1, XA_HEADS, XA_HEAD_DIM)
    s = jnp.einsum('blhd,bmhd->bhlm', q, k).astype(jnp.float32) * (XA_HEAD_DIM ** -0.5)
    p = jax.nn.softmax(s, axis=-1).astype(v.dtype)
    o = jnp.einsum('bhlm,bmhd->blhd', p, v).reshape(b, l, XA_WIDTH)
    return o @ wo


def setup_inputs(seed: int = 0) -> dict:
    key = jax.random.key(seed)
    ks = jax.random.split(key, 32)
    f32 = jnp.float32

    def nrm(k, shape, fan_in):
        return jax.random.normal(k, shape, f32) * (fan_in ** -0.5)

    x = jax.random.normal(ks[0], (BATCH, SEQ, D_MODEL), f32)
    mem = jax.random.normal(ks[1], (BATCH, N_MEM, D_MODEL), f32)
    offset = jax.random.randint(ks[2], (BATCH, 1), 0, 64, dtype=jnp.int32) * CHUNK
    positions = (offset + jnp.arange(SEQ, dtype=jnp.int32)[None, :]).astype(jnp.int32)
    norm_gains = 1.0 + 0.05 * jax.random.normal(ks[3], (DEPTH, N_GAINS, D_MODEL), f32)
    mem_norm_gain = 1.0 + 0.05 * jax.random.normal(ks[4], (DEPTH, D_MODEL), f32)
    ffn1_w_gu = nrm(ks[5], (DEPTH, D_MODEL, 2 * D_FF), D_MODEL)
    ffn1_w_down = nrm(ks[6], (DEPTH, D_FF, D_MODEL), D_FF)
    w_in = nrm(ks[7], (DEPTH, D_MODEL, IN_COLS), D_MODEL)
    attn_sinks = jax.random.normal(ks[8], (DEPTH, N_Q_HEADS), f32)
    w_attn_out = nrm(ks[9], (DEPTH, ATTN_WIDTH, D_MODEL), ATTN_WIDTH)
    ssm_lambda_re = -0.5 + 0.01 * jax.random.normal(ks[10], (DEPTH, SSM_GROUPS, SSM_STATE), f32)
    ssm_lambda_im = math.pi * jnp.arange(SSM_STATE, dtype=f32)[None, None, :] + 0.01 * jax.random.normal(ks[11], (DEPTH, SSM_GROUPS, SSM_STATE), f32)
    ssm_log_dt = jax.random.uniform(ks[12], (DEPTH, SSM_GROUPS), f32, minval=math.log(SSM_DT_MIN), maxval=math.log(SSM_DT_MAX))
    ssm_b_re = nrm(ks[13], (DEPTH, SSM_GROUPS, SSM_STATE, SSM_GROUP_CH), 2 * SSM_GROUP_CH)
    ssm_b_im = nrm(ks[14], (DEPTH, SSM_GROUPS, SSM_STATE, SSM_GROUP_CH), 2 * SSM_GROUP_CH)
    ssm_c_re = nrm(ks[15], (DEPTH, SSM_GROUPS, SSM_GROUP_CH, SSM_STATE), 2 * SSM_STATE)
    ssm_c_im = nrm(ks[16], (DEPTH, SSM_GROUPS, SSM_GROUP_CH, SSM_STATE), 2 * SSM_STATE)
    ssm_d = jax.random.normal(ks[17], (DEPTH, SSM_GROUPS, SSM_GROUP_CH), f32)
    w_glu = nrm(ks[18], (DEPTH, SSM_WIDTH, SSM_WIDTH), SSM_WIDTH)
    w_ssm_out = nrm(ks[19], (DEPTH, SSM_WIDTH, D_MODEL), SSM_WIDTH)
    w_o = nrm(ks[20], (DEPTH, D_MODEL, D_MODEL), D_MODEL)
    xa_wq = nrm(ks[21], (DEPTH, D_MODEL, XA_WIDTH), D_MODEL)
    xa_wkv = nrm(ks[22], (DEPTH, D_MODEL, 2 * XA_WIDTH), D_MODEL)
    xa_wo = nrm(ks[23], (DEPTH, XA_WIDTH, D_MODEL), XA_WIDTH)
    ffn2_w_gu = nrm(ks[24], (DEPTH, D_MODEL, 2 * D_FF), D_MODEL)
    ffn2_w_down = nrm(ks[25], (DEPTH, D_FF, D_MODEL), D_FF)
    return {'x': x, 'mem': mem, 'positions': positions, 'norm_gains': norm_gains,
            'mem_norm_gain': mem_norm_gain, 'ffn1_w_gu': ffn1_w_gu, 'ffn1_w_down': ffn1_w_down,
            'w_in': w_in, 'attn_sinks': attn_sinks, 'w_attn_out': w_attn_out,
            'ssm_lambda_re': ssm_lambda_re, 'ssm_lambda_im': ssm_lambda_im, 'ssm_log_dt': ssm_log_dt,
            'ssm_b_re': ssm_b_re, 'ssm_b_im': ssm_b_im, 'ssm_c_re': ssm_c_re, 'ssm_c_im': ssm_c_im,
            'ssm_d': ssm_d, 'w_glu': w_glu, 'w_ssm_out': w_ssm_out, 'w_o': w_o,
            'xa_wq': xa_wq, 'xa_wkv': xa_wkv, 'xa_wo': xa_wo,
            'ffn2_w_gu': ffn2_w_gu, 'ffn2_w_down': ffn2_w_down}


def reference(x, mem, positions, norm_gains, mem_norm_gain, ffn1_w_gu, ffn1_w_down, w_in, attn_sinks,
              w_attn_out, ssm_lambda_re, ssm_lambda_im, ssm_log_dt, ssm_b_re, ssm_b_im, ssm_c_re, ssm_c_im,
              ssm_d, w_glu, w_ssm_out, w_o, xa_wq, xa_wkv, xa_wo, ffn2_w_gu, ffn2_w_down):
    b, l = x.shape[:2]
    cos, sin = rope_tables(positions)
    c0 = ATTN_WIDTH
    c1 = c0 + KV_WIDTH
    c2 = c1 + KV_WIDTH
    c3 = c2 + SSM_WIDTH
    c4 = c3 + D_MODEL
    h = x
    for i in range(DEPTH):
        g = norm_gains[i]
        h = h + 0.5 * rmsnorm(swiglu(rmsnorm(h, g[0]), ffn1_w_gu[i], ffn1_w_down[i]), g[1])
        u = rmsnorm(h, g[2])
        q, k, v, s_in, gate_a, gate_s = jnp.split(u @ w_in[i], [c0, c1, c2, c3, c4], axis=-1)
        q = apply_partial_rope(q.reshape(b, l, N_Q_HEADS, HEAD_DIM), cos, sin)
        k = apply_partial_rope(k.reshape(b, l, N_KV_HEADS, HEAD_DIM), cos, sin)
        v = v.reshape(b, l, N_KV_HEADS, HEAD_DIM)
        y_attn = sliding_window_gqa_with_sinks(q, k, v, attn_sinks[i]) @ w_attn_out[i]
        y_ssm = s5_glu_branch(s_in, ssm_lambda_re[i], ssm_lambda_im[i], ssm_log_dt[i], ssm_b_re[i], ssm_b_im[i],
                              ssm_c_re[i], ssm_c_im[i], ssm_d[i], w_glu[i]) @ w_ssm_out[i]
        mixed = (jax.nn.sigmoid(gate_a) * y_attn + jax.nn.sigmoid(gate_s) * y_ssm) @ w_o[i]
        h = h + rmsnorm(mixed, g[3])
        memn = rmsnorm(mem, mem_norm_gain[i])
        h = h + rmsnorm(memory_cross_attention(rmsnorm(h, g[4]), memn, xa_wq[i], xa_wkv[i], xa_wo[i]), g[5])
        h = h + 0.5 * rmsnorm(swiglu(rmsnorm(h, g[6]), ffn2_w_gu[i], ffn2_w_down[i]), g[7])
    return h
```

```python
import numpy as np
import concourse.bass as bass
import concourse.mybir as mybir
from concourse.bass_utils import run_bass_kernel_spmd

F32 = mybir.dt.float32
BF16 = mybir.dt.bfloat16
I32 = mybir.dt.int32
AF = mybir.ActivationFunctionType
ALU = mybir.AluOpType

D = 4096
DEPTH = 4
NT_FULL = 16384
TT = 512
NMEM = 256
EPS = 1e-6
TWO_PI = 2.0 * np.pi


class Sem:
    def __init__(self, h):
        self.h = h
        self.count = 0


class Buf:
    def __init__(self, t=None):
        self.t = t
        self.w = []
        self.r = []


class KB:
    ENG = ("sp", "act", "dve", "pe", "pool")

    def __init__(self, nc, sems):
        self.nc = nc
        self.free_sems = list(sems)
        self.ops = {e: [] for e in self.ENG}
        self.known = {e: {} for e in self.ENG}
        self.prog = {}
        self.all_sems = []
        self.dma_pool = {"sp": [self.new_sem() for _ in range(28)], "pool": [self.new_sem() for _ in range(6)],
                         "act": [self.new_sem() for _ in range(6)]}
        self.dma_i = {"sp": 0, "pool": 0, "act": 0}
        self.dram = {}

    def new_sem(self):
        s = Sem(self.free_sems.pop())
        self.all_sems.append(s)
        return s

    def prog_sem(self, eng):
        s = self.prog.get(eng)
        if s is None or s.count >= 30000:
            s = self.new_sem()
            self.prog[eng] = s
        return s

    def _record(self, eng, fn, waits, inc):
        kn = self.known[eng]
        ws = []
        for (s, v) in waits:
            if v <= 0:
                continue
            if kn.get(s, 0) >= v:
                continue
            kn[s] = v
            ws.append((s, v))
        self.ops[eng].append((ws, fn, inc))

    def op(self, eng, fn, reads=(), writes=(), mark=True, extra=(), acc=False):
        waits = list(extra)
        for b in reads:
            waits += b.w
        for b in writes:
            waits += b.w + b.r
        tok = None
        inc = None
        if mark:
            s = self.prog_sem(eng)
            s.count += 1
            tok = (s, s.count)
            inc = (s, 1)
        self._record(eng, fn, waits, inc)
        if tok is not None:
            for b in reads:
                b.r.append(tok)
            for b in writes:
                if acc:
                    b.w.append(tok)
                else:
                    b.w = [tok]
                    b.r = []
        return tok

    def mm_group(self, pb, pairs, reads=(), transpose=False):
        n = len(pairs)
        tok = None
        for i, (l, r) in enumerate(pairs):
            first, last = (i == 0), (i == n - 1)
            fn = (lambda e, l=l, r=r, first=first, last=last: e.matmul(pb.t[:] if not hasattr(pb, 'view') else pb.view, l, r, start=first, stop=last))
            tok = self.op("pe", fn, reads=list(reads) if first else (), writes=[pb] if first else (), mark=last or first)
        pb.w = [tok]
        pb.r = []
        return tok

    def dma(self, eng, out_ap, in_ap, reads=(), writes=(), extra=(), acc=False):
        pool = self.dma_pool[eng]
        s = pool[self.dma_i[eng] % len(pool)]
        self.dma_i[eng] += 1
        waits = list(extra) + [(s, s.count)]
        for b in reads:
            waits += b.w
        for b in writes:
            waits += b.w + b.r
        s.count += 16
        tok = (s, s.count)
        self._record(eng, lambda e: e.dma_start(out=out_ap, in_=in_ap), waits, (s, 16))
        for b in reads:
            b.r.append(tok)
        for b in writes:
            if acc:
                b.w.append(tok)
            else:
                b.w = [tok]
                b.r = []
        return tok

    def dbuf(self, name, idx):
        key = (name, idx)
        b = self.dram.get(key)
        if b is None:
            b = Buf()
            self.dram[key] = b
        return b

    def barrier(self):
        waits = [(s, s.count) for s in self.all_sems if s.count > 0]
        for e in self.ENG:
            self._record(e, None, waits, None)

    def replay(self, eng, e):
        for (ws, fn, inc) in self.ops[eng]:
            for (s, v) in ws:
                e.wait_ge(s.h, v)
            if fn is None:
                continue
            ins = fn(e)
            if inc is not None:
                ins.then_inc(inc[0].h, inc[1])


def tile_w(w, pair_gu=False):
    K, OUT = w.shape
    KC = K // 128
    if pair_gu:
        half = OUT // 2
        nt = half // 128
        g = w[:, :half].reshape(KC, 128, nt, 128)
        u = w[:, half:].reshape(KC, 128, nt, 128)
        t = np.concatenate([g, u], axis=3)
    else:
        nt = OUT // 256
        t = w.reshape(KC, 128, nt, 256)
    t = np.ascontiguousarray(t.transpose(2, 1, 0, 3)).reshape(nt * 128, KC * 256)
    return t


WSPEC = [
    ("ffn1_w_gu", 4096, 8192), ("ffn1_w_down", 4096, 4096), ("w_in", 4096, 12800), ("w_attn_out", 2048, 4096),
    ("w_glu", 2048, 2048), ("w_ssm_out", 2048, 4096), ("w_o", 4096, 4096), ("xa_wq", 4096, 512),
    ("xa_wkv", 4096, 1024), ("xa_wo", 512, 4096), ("ffn2_w_gu", 4096, 8192), ("ffn2_w_down", 4096, 4096),
]


def build(NT, depth, stop_after=None, dbg=()):
    nc = bass.Bass("TRN2", target_bir_lowering=False)
    NTT = NT // TT

    def din(name, shape, dt=F32):
        return nc.dram_tensor(name, list(shape), dt, kind="ExternalInput").ap()

    def dscr(name, shape, dt=F32):
        if name in dbg:
            return nc.dram_tensor(name, list(shape), dt, kind="ExternalOutput").ap()
        return nc.dram_tensor(name, list(shape), dt).ap()

    xT = din("xT", [D, NT])
    memT = din("memT", [D, NMEM])
    pos = din("pos", [1, NT], I32)
    gains = din("gains", [depth * 128, 8 * 32])
    mgain = din("mgain", [depth * 128, 32])
    Wf = {}
    Wb = {}
    for (n, K, OUT) in WSPEC:
        nt = OUT // 256 if "gu" not in n else OUT // 256
        Wf[n] = din(n, [depth * nt * 128, (K // 128) * 256])
        Wb[n] = [dscr(n + "_b%d" % l, [nt * 128, (K // 128) * 256], BF16) for l in range(depth)]
    sinks = din("sinks", [depth * 128, 16])
    lamre = din("lamre", [depth * 128, 64])
    lamim = din("lamim", [depth * 128, 64])
    logdt = din("logdt", [depth * 128, 64])
    bre = din("bre", [depth * 128, 64 * 16])
    bim = din("bim", [depth * 128, 64 * 16])
    cre = din("cre", [depth * 128, 64 * 16])
    cim = din("cim", [depth * 128, 64 * 16])
    dsk = din("dsk", [depth * 32, 64])
    c_ident = din("c_ident", [128, 128])
    c_rotT = din("c_rotT", [32, 32])
    c_invf = din("c_invf", [32, 1])
    c_j1 = din("c_j1", [128, 512])
    c_mask = din("c_mask", [128, 2 * 256])
    yT = nc.dram_tensor("yT", [D, NT], F32, kind="ExternalOutput").ap()

    hT = dscr("hT", [D, NT])
    qT = dscr("qT", [2048, NT], BF16)
    kT = dscr("kT", [256, NT], BF16)
    vT = dscr("vT", [256, NT], BF16)
    sT = dscr("sT", [2048, NT])
    gaT = dscr("gaT", [D, NT])
    gsT = dscr("gsT", [D, NT])
    oT = dscr("oT", [2048, NT], BF16)
    ygT = dscr("ygT", [2048, NT], BF16)
    cosT = dscr("cosT", [32, NT])
    sinT = dscr("sinT", [32, NT])
    xqT = dscr("xqT", [512, NT], BF16)
    xoT = dscr("xoT", [512, NT], BF16)
    kvmT = dscr("kvmT", [1024, NMEM], BF16)

    from contextlib import ExitStack
    es = ExitStack()

    def sb(name, shape, dt):
        return es.enter_context(nc.sbuf_tensor(name, list(shape), dt))

    def ps(name, shape=(128, 512), dt=F32):
        return es.enter_context(nc.psum_tensor(name, list(shape), dt))

    sems = [es.enter_context(nc.semaphore("s%d" % i)) for i in range(100)]
    k = KB(nc, sems)

    A32 = sb("A32", [128, 32, 512], F32)
    X16 = sb("X16", [128, 32, 512], BF16)
    H16 = sb("H16", [128, 32, 512], BF16)
    WB = [sb("WB%d" % i, [128, 32 * 256], BF16) for i in range(2)]
    SQ = [sb("SQ%d" % i, [128, 2, 512], BF16) for i in range(2)]
    T32 = [sb("T32_%d" % i, [128, 512], F32) for i in range(6)]
    HC = [sb("HC%d" % i, [128, 2, 512], F32) for i in range(2)]
    OC = [sb("OC%d" % i, [128, 2, 512], F32) for i in range(2)]
    GN = sb("GN", [128, 8 * 32], F32)
    MG = sb("MG", [128, 32], F32)
    ONES = sb("ONES", [128, 128], BF16)
    IDENT = sb("IDENT", [128, 128], F32)
    IDB = sb("IDB", [128, 128], BF16)
    CST = sb("CST", [128, 8], F32)
    PS = [ps("PS%d" % i) for i in range(6)]

    bA32, bX16, bH16 = Buf(A32), Buf(X16), Buf(H16)
    bWB = [Buf(t) for t in WB]
    bSQ = [Buf(t) for t in SQ]
    bT32 = [Buf(t) for t in T32]
    bHC = [Buf(t) for t in HC]
    bOC = [Buf(t) for t in OC]
    bGN, bMG, bONES, bIDENT, bIDB, bCST = Buf(GN), Buf(MG), Buf(ONES), Buf(IDENT), Buf(IDB), Buf(CST)
    bPS = [Buf(t) for t in PS]
    rr = {"ps": 0, "wb": 0, "sq": 0, "hc": 0, "oc": 0}

    def nxt(kind, lst):
        i = rr[kind] % len(lst)
        rr[kind] += 1
        return lst[i]

    k.dma("sp", IDENT[:], c_ident[:, :], writes=[bIDENT])
    k.op("dve", lambda e: e.tensor_copy(out=IDB[:], in_=IDENT[:]), reads=[bIDENT], writes=[bIDB])
    k.op("dve", lambda e: e.memset(ONES[:], 1.0), writes=[bONES])
    k.op("dve", lambda e: e.memset(CST[:, 0:1], EPS), writes=[bCST])

    wtok = {}
    for l in range(depth):
        for (n, K, OUT) in WSPEC:
            nt = OUT // 256
            toks = []
            for j in range(nt):
                r0 = (l * nt + j) * 128
                toks.append(k.dma("pool", Wb[n][l][j * 128:(j + 1) * 128, :], Wf[n][r0:r0 + 128, :]))
            wtok[(n, l)] = toks

    PSB = [es.enter_context(nc.psum_tensor("PSB%d" % i, [128, 256], BF16)) for i in range(2)]
    bPSB = [Buf(t) for t in PSB]
    Q16 = [sb("Q16_%d" % i, [128, 512], BF16) for i in range(2)]
    bQ16 = [Buf(t) for t in Q16]
    SM = [sb("SM%d" % i, [128, 8], F32) for i in range(2)]
    bSM = [Buf(t) for t in SM]
    rr["psb"] = 0; rr["q16"] = 0; rr["sm"] = 0; rr["t32"] = 0
    k.op("dve", lambda e: e.memset(CST[:, 1:2], float(np.pi)), writes=[bCST], acc=True)

    def mmg(pb, out_ap, pairs, reads):
        n = len(pairs)
        tok = None
        for i, (l, r) in enumerate(pairs):
            first, last = (i == 0), (i == n - 1)
            tok = k.op("pe", lambda e, l=l, r=r, first=first, last=last: e.matmul(out_ap, l, r, start=first, stop=last),
                       reads=list(reads) if (first or last) else (), writes=[pb] if first else (), mark=(first or last))
        pb.w = [tok]
        pb.r = []
        return tok

    def sumsq_rstd(src, bsrc, nch, scale_out, t1, ncol=TT):
        pb = nxt("ps", bPS)
        n4 = nch // 2
        for g in range(n4):
            sq = nxt("sq", bSQ)
            k.op("act", lambda e, g=g, sq=sq: e.activation(out=sq.t[:, :, 0:ncol], in_=src[:, 2 * g:2 * g + 2, 0:ncol], func=AF.Square),
                 reads=[bsrc], writes=[sq])
            for c in range(2):
                first = (g == 0 and c == 0)
                last = (g == n4 - 1 and c == 1)
                k.op("pe", lambda e, sq=sq, c=c, first=first, last=last: e.matmul(pb.t[:, 0:ncol], ONES[:], sq.t[:, c, 0:ncol], start=first, stop=last),
                     reads=[sq, bONES], writes=[pb] if first else (), mark=(c == 1 or first), acc=not first)
        pb.w = [(k.prog["pe"], k.prog["pe"].count)]
        pb.r = []
        k.op("dve", lambda e: e.tensor_scalar(out=t1.t[:, 0:ncol], in0=pb.t[:, 0:ncol], scalar1=1.0 / (nch * 128), scalar2=EPS,
                                              op0=ALU.mult, op1=ALU.add), reads=[pb], writes=[t1])
        k.op("act", lambda e: e.activation(out=t1.t[:, 0:ncol], in_=t1.t[:, 0:ncol], func=AF.Sqrt), writes=[t1])
        k.op("dve", lambda e: e.reciprocal(out=t1.t[:, 0:ncol], in_=t1.t[:, 0:ncol]), writes=[t1])
        if scale_out != 1.0:
            k.op("dve", lambda e: e.tensor_scalar(out=t1.t[:, 0:ncol], in0=t1.t[:, 0:ncol], scalar1=float(scale_out), scalar2=None,
                                                  op0=ALU.mult), writes=[t1])
        return t1

    def norm_in(src, sname, tt, gtile, gcol0, ncol=TT):
        t0 = tt * ncol
        for q in range(4):
            k.dma("sp", A32[:, 8 * q:8 * q + 8, 0:ncol],
                  src[q * 1024:(q + 1) * 1024, t0:t0 + ncol].rearrange("(kc p) t -> p kc t", p=128),
                  reads=[k.dbuf(sname, tt)], writes=[bA32], acc=(q > 0))
        rstd = sumsq_rstd(A32, bA32, 32, 1.0, bT32[0], ncol)
        for kc in range(32):
            k.op("dve", lambda e, kc=kc: e.scalar_tensor_tensor(out=X16[:, kc, 0:ncol], in0=A32[:, kc, 0:ncol],
                                                               scalar=gtile[:, gcol0 + kc:gcol0 + kc + 1], in1=rstd.t[:, 0:ncol],
                                                               op0=ALU.mult, op1=ALU.mult),
                 reads=[bA32, rstd, bGN, bMG], writes=[bX16], acc=(kc > 0))

    def gemm(Xt, bX, KC, wname, l, nt_layer, jlist, epi, ncol=TT):
        for j in jlist:
            wb = nxt("wb", bWB)
            r0 = (l * nt_layer + j) * 128
            k.dma("sp", wb.t[:, 0:KC * 256], Wb[wname][l][j * 128:(j + 1) * 128, :], writes=[wb], extra=[wtok[(wname, l)][j]])
            for half in range(2):
                pb = nxt("ps", bPS)
                mmg(pb, pb.t[:, 0:ncol],
                    [(wb.t[:, kc * 256 + half * 128:kc * 256 + half * 128 + 128], Xt[:, kc, 0:ncol]) for kc in range(KC)],
                    [wb, bX])
                epi(j, half, pb)

    def post_res(l, gcol0, coef, hsrc, sname_src, hdst, sname_dst, tt):
        t0 = tt * TT
        rstd = sumsq_rstd(A32, bA32, 32, coef, bT32[1])
        for o2 in range(16):
            hc = nxt("hc", bHC)
            oc_ = nxt("oc", bOC)
            k.dma("sp", hc.t[:], hsrc[o2 * 256:(o2 + 1) * 256, t0:t0 + TT].rearrange("(c p) t -> p c t", p=128),
                  reads=[k.dbuf(sname_src, tt)], writes=[hc])
            for c in range(2):
                oc = 2 * o2 + c
                k.op("dve", lambda e, oc=oc, c=c, oc_=oc_: e.scalar_tensor_tensor(
                    out=oc_.t[:, c, :], in0=A32[:, oc, :], scalar=GN[:, gcol0 + oc:gcol0 + oc + 1], in1=rstd.t[:],
                    op0=ALU.mult, op1=ALU.mult), reads=[bA32, rstd, bGN], writes=[oc_], acc=(c > 0))
            k.op("dve", lambda e, oc_=oc_, hc=hc: e.tensor_tensor(out=oc_.t[:], in0=oc_.t[:], in1=hc.t[:], op=ALU.add),
                 reads=[hc], writes=[oc_])
            k.dma("pool", hdst[o2 * 256:(o2 + 1) * 256, t0:t0 + TT].rearrange("(c p) t -> p c t", p=128), oc_.t[:],
                  reads=[oc_], writes=[k.dbuf(sname_dst, tt)], acc=(o2 > 0))

    def copy_to_A32(j, half, pb):
        oc = 2 * j + half
        k.op("act", lambda e: e.activation(out=A32[:, oc, :], in_=pb.t[:], func=AF.Copy), reads=[pb], writes=[bA32], acc=True)

    def load_gains(l):
        k.dma("sp", GN[:], gains[l * 128:(l + 1) * 128, :], writes=[bGN])
        k.dma("sp", MG[:], mgain[l * 128:(l + 1) * 128, :], writes=[bMG])

    def ffn_stage(l, which, gpre, gpost, hsrc, sname_src, hdst, sname_dst):
        wgu, wdn = which + "_w_gu", which + "_w_down"
        for tt in range(NTT):
            norm_in(hsrc, sname_src, tt, GN, gpre * 32)
            st = {}

            def epi1(j, half, pb):
                if half == 0:
                    st["g"] = pb
                    return
                pg = st["g"]
                t = bT32[2 + (j % 2)]
                k.op("act", lambda e: e.activation(out=t.t[:], in_=pg.t[:], func=AF.Silu), reads=[pg], writes=[t])
                k.op("dve", lambda e: e.tensor_tensor(out=H16[:, j, :], in0=t.t[:], in1=pb.t[:], op=ALU.mult),
                     reads=[t, pb], writes=[bH16], acc=True)
            gemm(X16, bX16, 32, wgu, l, 32, range(32), epi1)
            bA32.w = list(bA32.w)
            gemm(H16, bH16, 32, wdn, l, 16, range(16), copy_to_A32)
            post_res(l, gpost * 32, 0.5, hsrc, sname_src, hdst, sname_dst, tt)
            bH16.w, bH16.r = list(bH16.w), list(bH16.r)

    C1 = 6.28125
    C2 = float(2.0 * np.pi - 6.28125)
    PIF = 3.1415925

    def sin_reduced(dst, bdst, ang, bang, ti, bti, tf, btf, shift=0.0):
        if shift != 0.0:
            k.op("dve", lambda e: e.tensor_scalar(out=dst, in0=ang, scalar1=float(shift), scalar2=None, op0=ALU.add),
                 reads=[bang], writes=[bdst])
            src, bsrc = dst, bdst
        else:
            src, bsrc = ang, bang
        k.op("dve", lambda e: e.tensor_scalar(out=ti, in0=src, scalar1=float(1.0 / (2.0 * np.pi)), scalar2=None, op0=ALU.mult),
             reads=[bsrc], writes=[bti])
        k.op("dve", lambda e: e.tensor_copy(out=tf, in_=ti), reads=[bti], writes=[btf])
        k.op("dve", lambda e: e.scalar_tensor_tensor(out=dst, in0=tf, scalar=-C1, in1=src, op0=ALU.mult, op1=ALU.add),
             reads=[btf, bsrc], writes=[bdst])
        k.op("dve", lambda e: e.scalar_tensor_tensor(out=dst, in0=tf, scalar=-C2, in1=dst, op0=ALU.mult, op1=ALU.add),
             reads=[btf], writes=[bdst])
        k.op("dve", lambda e: e.tensor_scalar(out=dst, in0=dst, scalar1=-PIF, scalar2=PIF, op0=ALU.max, op1=ALU.min),
             writes=[bdst])
        k.op("act", lambda e: e.activation(out=dst, in_=dst, func=AF.Sin), writes=[bdst])

    def rope_tables():
        PI = CS[:, 0, :].bitcast(I32)
        bPI = Buf()
        INVF = sb("INVF", [32, 1], F32)
        bINVF = Buf(INVF)
        k.dma("sp", INVF[:], c_invf[:, :], writes=[bINVF])
        for tt in range(NTT):
            t0 = tt * TT
            k.dma("sp", PI, pos[0:1, t0:t0 + TT].partition_broadcast(32), writes=[bPI])
            a, b_, c_, d_ = bT32[2], bT32[3], bT32[4], bT32[5]
            k.op("dve", lambda e: e.tensor_copy(out=a.t[0:32, :], in_=PI), reads=[bPI], writes=[a])
            k.op("dve", lambda e: e.tensor_scalar(out=a.t[0:32, :], in0=a.t[0:32, :], scalar1=INVF[:, 0:1], scalar2=None, op0=ALU.mult),
                 reads=[bINVF], writes=[a])
            sin_reduced(b_.t[0:32, :], b_, a.t[0:32, :], a, PI, bPI, d_.t[0:32, :], d_)
            k.dma("pool", sinT[:, t0:t0 + TT], b_.t[0:32, :], reads=[b_], writes=[k.dbuf("sinT", tt)])
            sin_reduced(c_.t[0:32, :], c_, a.t[0:32, :], a, PI, bPI, d_.t[0:32, :], d_, shift=float(np.pi / 2))
            k.dma("pool", cosT[:, t0:t0 + TT], c_.t[0:32, :], reads=[c_], writes=[k.dbuf("cosT", tt)])

    ROTF = sb("ROTF", [32, 32], F32)
    ROTB = sb("ROTB", [32, 32], BF16)
    bROT = Buf(ROTB)
    k.dma("sp", ROTF[:], c_rotT[:, :], writes=[bROT])
    k.op("dve", lambda e: e.tensor_copy(out=ROTB[:], in_=ROTF[:]), writes=[bROT])
    CS = sb("CS", [32, 2, 512], F32)
    bCS = Buf(CS)
    QB = [sb("QB%d" % i, [32, 512], BF16) for i in range(2)]
    bQB = [Buf(t) for t in QB]
    rr["qb"] = 0

    def mixer_proj(l, hsrc, sname):
        for tt in range(NTT):
            t0 = tt * TT
            norm_in(hsrc, sname, tt, GN, 2 * 32)
            k.dma("sp", CS[:, 0, :], cosT[:, t0:t0 + TT], reads=[k.dbuf("cosT", tt)], writes=[bCS])
            k.dma("sp", CS[:, 1, :], sinT[:, t0:t0 + TT], reads=[k.dbuf("sinT", tt)], writes=[bCS], acc=True)

            def epi(j, half, pb):
                col = j * 256 + half * 128
                if col < 2304:
                    tq = nxt("t32", bT32[2:6])
                    qb = nxt("qb", bQB)
                    k.op("act", lambda e: e.activation(out=tq.t[:], in_=pb.t[:], func=AF.Copy), reads=[pb], writes=[tq])
                    k.op("act", lambda e: e.activation(out=qb.t[:], in_=pb.t[0:32, :], func=AF.Copy), reads=[pb], writes=[qb])
                    pr = nxt("ps", bPS)
                    mmg(pr, pr.t[0:32, :], [(ROTB[:], qb.t[:])], [qb, bROT])
                    ts = nxt("t32", bT32[2:6])
                    k.op("dve", lambda e: e.tensor_tensor(out=ts.t[0:32, :], in0=pr.t[0:32, :], in1=CS[:, 1, :], op=ALU.mult),
                         reads=[pr, bCS], writes=[ts])
                    k.op("dve", lambda e: e.tensor_tensor(out=tq.t[0:32, :], in0=tq.t[0:32, :], in1=CS[:, 0, :], op=ALU.mult),
                         reads=[bCS], writes=[tq])
                    k.op("dve", lambda e: e.tensor_tensor(out=tq.t[0:32, :], in0=tq.t[0:32, :], in1=ts.t[0:32, :], op=ALU.add),
                         reads=[ts], writes=[tq])
                    q16 = nxt("q16", bQ16)
                    k.op("act", lambda e: e.activation(out=q16.t[:], in_=tq.t[:], func=AF.Copy), reads=[tq], writes=[q16])
                    if col < 2048:
                        k.dma("pool", qT[col:col + 128, t0:t0 + TT], q16.t[:], reads=[q16], writes=[k.dbuf("qT", tt)], acc=True)
                    else:
                        c2 = col - 2048
                        k.dma("pool", kT[c2:c2 + 128, t0:t0 + TT], q16.t[:], reads=[q16], writes=[k.dbuf("kT", tt)], acc=True)
                elif col < 2560:
                    q16 = nxt("q16", bQ16)
                    c2 = col - 2304
                    k.op("act", lambda e: e.activation(out=q16.t[:], in_=pb.t[:], func=AF.Copy), reads=[pb], writes=[q16])
                    k.dma("pool", vT[c2:c2 + 128, t0:t0 + TT], q16.t[:], reads=[q16], writes=[k.dbuf("vT", tt)], acc=True)
                else:
                    oc_ = nxt("oc", bOC)
                    if col < 4608:
                        dst, nm, c2, fn = sT, "sT", col - 2560, AF.Copy
                    elif col < 8704:
                        dst, nm, c2, fn = gaT, "gaT", col - 4608, AF.Sigmoid
                    else:
                        dst, nm, c2, fn = gsT, "gsT", col - 8704, AF.Sigmoid
                    k.op("act", lambda e: e.activation(out=oc_.t[:, 0, :], in_=pb.t[:], func=fn), reads=[pb], writes=[oc_])
                    k.dma("pool", dst[c2:c2 + 128, t0:t0 + TT], oc_.t[:, 0, :], reads=[oc_], writes=[k.dbuf(nm, tt)], acc=True)
            gemm(X16, bX16, 32, "w_in", l, 50, range(50), epi)
    A32f = A32[:].rearrange("p a b -> p (a b)")
    X16f = X16[:].rearrange("p a b -> p (a b)")
    H16f = H16[:].rearrange("p a b -> p (a b)")
    KM = sb("KM", [128, 4, 256], BF16)
    VM = sb("VM", [128, 8, 128], BF16)
    bKM, bVM = Buf(KM), Buf(VM)
    SCALE = float(128 ** -0.5)

    def attn_unit(q_ap, bq, k_ap, bk, v0_ap, v1_ap, bv, mask_ap, sink_ap, bms, out_ap, bout):
        pb = nxt("ps", bPS)
        mmg(pb, pb.t[:, 0:256], [(q_ap, k_ap)], [bq, bk])
        s1 = nxt("t32", bT32[2:6])
        sm = nxt("sm", bSM)
        if mask_ap is not None:
            k.op("dve", lambda e: e.scalar_tensor_tensor(out=s1.t[:, 0:256], in0=pb.t[:, 0:256], scalar=SCALE, in1=mask_ap,
                                                         op0=ALU.mult, op1=ALU.add), reads=[pb, bms], writes=[s1])
        else:
            k.op("dve", lambda e: e.tensor_scalar(out=s1.t[:, 0:256], in0=pb.t[:, 0:256], scalar1=SCALE, scalar2=None, op0=ALU.mult),
                 reads=[pb], writes=[s1])
        k.op("dve", lambda e: e.reduce_max(out=sm.t[:, 0:1], in_=s1.t[:, 0:256], axis=mybir.AxisListType.X), reads=[s1], writes=[sm])
        if sink_ap is not None:
            k.op("dve", lambda e: e.tensor_tensor(out=sm.t[:, 0:1], in0=sm.t[:, 0:1], in1=sink_ap, op=ALU.max), reads=[bms], writes=[sm])
        k.op("dve", lambda e: e.tensor_scalar(out=sm.t[:, 1:2], in0=sm.t[:, 0:1], scalar1=-1.0, scalar2=None, op0=ALU.mult), writes=[sm])
        k.op("dve", lambda e: e.memset(sm.t[:, 2:4], 0.0), writes=[sm])
        k.op("act", lambda e: e.activation(out=s1.t[:, 0:256], in_=s1.t[:, 0:256], func=AF.Exp, bias=sm.t[:, 1:2], accum_out=sm.t[:, 2:3]),
             reads=[sm], writes=[s1, sm])
        if sink_ap is not None:
            k.op("act", lambda e: e.activation(out=sm.t[:, 3:4], in_=sink_ap, func=AF.Exp, bias=sm.t[:, 1:2]), reads=[bms], writes=[sm])
        k.op("dve", lambda e: e.tensor_tensor(out=sm.t[:, 4:5], in0=sm.t[:, 2:3], in1=sm.t[:, 3:4], op=ALU.add), writes=[sm])
        k.op("dve", lambda e: e.reciprocal(out=sm.t[:, 5:6], in_=sm.t[:, 4:5]), writes=[sm])
        pbf = nxt("q16", bQ16)
        k.op("dve", lambda e: e.tensor_scalar(out=pbf.t[:, 0:256], in0=s1.t[:, 0:256], scalar1=sm.t[:, 5:6], scalar2=None, op0=ALU.mult),
             reads=[s1, sm], writes=[pbf])
        pt = nxt("psb", bPSB)
        k.op("pe", lambda e: e.transpose(pt.t[:, 0:128], pbf.t[:, 0:128], IDB[:]), reads=[pbf, bIDB], writes=[pt])
        k.op("pe", lambda e: e.transpose(pt.t[:, 128:256], pbf.t[:, 128:256], IDB[:]), reads=[pbf, bIDB], writes=[pt], acc=True)
        k.op("act", lambda e: e.activation(out=pbf.t[:, 256:512], in_=pt.t[:, 0:256], func=AF.Copy), reads=[pt], writes=[pbf])
        po = nxt("ps", bPS)
        mmg(po, po.t[:, 0:128], [(v0_ap, pbf.t[:, 256:384]), (v1_ap, pbf.t[:, 384:512])], [bv, pbf])
        k.op("act", lambda e: e.activation(out=out_ap, in_=po.t[:, 0:128], func=AF.Copy), reads=[po], writes=[bout], acc=True)

    def attention_stage(l):
        MSK = A32f[:, 0:512]
        SNK = A32f[:, 512:528]
        bMS = Buf()
        k.dma("sp", MSK, c_mask[:, :], writes=[bMS])
        k.dma("sp", SNK, sinks[l * 128:(l + 1) * 128, :], writes=[bMS], acc=True)
        KH = [X16f[:, kv * 640:(kv + 1) * 640] for kv in range(2)]
        VT = [X16f[:, 1280 + kv * 640:1280 + (kv + 1) * 640] for kv in range(2)]
        VB = [[X16f[:, 2560 + (kv * 5 + b) * 128:2560 + (kv * 5 + b + 1) * 128] for b in range(5)] for kv in range(2)]
        bKH, bVT, bVB = Buf(), Buf(), Buf()
        Q = H16[:, 0:16, :]
        O = H16[:, 16:32, :]
        bQ, bO = Buf(), Buf()
        for tt in range(NTT):
            t0 = tt * TT
            for kv in range(2):
                first = (kv == 0)
                if tt == 0:
                    k.op("dve", lambda e, kv=kv: e.memset(KH[kv][:, 0:128], 0.0), writes=[bKH], acc=not first)
                    k.op("dve", lambda e, kv=kv: e.memset(VT[kv][:, 0:128], 0.0), writes=[bVT], acc=not first)
                    k.dma("sp", KH[kv][:, 128:640], kT[kv * 128:(kv + 1) * 128, t0:t0 + TT], reads=[k.dbuf("kT", tt)], writes=[bKH], acc=True)
                    k.dma("sp", VT[kv][:, 128:640], vT[kv * 128:(kv + 1) * 128, t0:t0 + TT], reads=[k.dbuf("vT", tt)], writes=[bVT], acc=True)
                else:
                    k.dma("sp", KH[kv][:, 0:640], kT[kv * 128:(kv + 1) * 128, t0 - 128:t0 + TT],
                          reads=[k.dbuf("kT", tt), k.dbuf("kT", tt - 1)], writes=[bKH], acc=not first)
                    k.dma("sp", VT[kv][:, 0:640], vT[kv * 128:(kv + 1) * 128, t0 - 128:t0 + TT],
                          reads=[k.dbuf("vT", tt), k.dbuf("vT", tt - 1)], writes=[bVT], acc=not first)
            k.dma("sp", Q, qT.rearrange("(h d) t -> d h t", d=128)[:, :, t0:t0 + TT], reads=[k.dbuf("qT", tt)], writes=[bQ])
            n = 0
            for kv in range(2):
                for b in range(5):
                    pt = nxt("psb", bPSB)
                    k.op("pe", lambda e, kv=kv, b=b, pt=pt: e.transpose(pt.t[:, 0:128], VT[kv][:, b * 128:(b + 1) * 128], IDB[:]),
                         reads=[bVT, bIDB], writes=[pt])
                    k.op("act", lambda e, kv=kv, b=b, pt=pt: e.activation(out=VB[kv][b], in_=pt.t[:, 0:128], func=AF.Copy),
                         reads=[pt], writes=[bVB], acc=(n > 0))
                    n += 1
            bO.w = list(bO.w)
            for b in range(4):
                for h in range(16):
                    kv = h // 8
                    m0 = 256 if (tt == 0 and b == 0) else 0
                    attn_unit(Q[:, h, b * 128:(b + 1) * 128], bQ, KH[kv][:, b * 128:b * 128 + 256], bKH,
                              VB[kv][b], VB[kv][b + 1], bVB, MSK[:, m0:m0 + 256], SNK[:, h:h + 1], bMS,
                              O[:, h, b * 128:(b + 1) * 128], bO)
            k.dma("pool", oT.rearrange("(h d) t -> d h t", d=128)[:, :, t0:t0 + TT], O, reads=[bO], writes=[k.dbuf("oT", tt)])
            bO.w = []

    def ssm_stage(l):
        r0 = l * 128
        fcol = [0]

        def fa(n):
            v = A32f[:, fcol[0]:fcol[0] + n]
            fcol[0] += n
            return v
        LRE, LIM, DT, TH, RR, SN, CSN, LBR, LBI, DEN, KRE, KIM, TMPA, TMPB = [fa(64) for _ in range(14)]
        TI = fa(64).bitcast(I32)
        BRE, BIM, BBR, BBI, CRE, CIM = [fa(1024) for _ in range(6)]
        BPR, BPI = fa(2048), fa(2048)
        TBC, TBS, RT, J1 = fa(512), fa(512), fa(512), fa(512)
        TIJ = fa(512).bitcast(I32)
        TFJ = fa(512)
        DSK = fa(64)
        assert fcol[0] <= 16384, fcol[0]
        XS = [[BRE[:, 0:512], BRE[:, 512:1024]], [BIM[:, 0:512], BIM[:, 512:1024]]]
        W4 = [BBR[:, 0:512], BBR[:, 512:1024], BBI[:, 0:512], BBI[:, 512:1024]]
        bP = Buf()
        bTAB = Buf()
        bXS = [Buf(), Buf()]
        bW4 = [Buf() for _ in range(4)]
        LBre = H16f[0:32, 0:8192]
        LBim = H16f[0:32, 8192:16384]
        CPr = X16f[:, 0:2048]
        CPn = X16f[:, 2048:4096]
        XBr = X16f[:, 4096:4608]
        XBi = X16f[:, 4608:5120]
        bLB, bCP, bXB = Buf(), Buf(), Buf()

        def dv(fn, reads=(), writes=(), acc=False):
            return k.op("dve", fn, reads=reads, writes=writes, acc=acc)

        def tt_(out, a, b, op, reads=(), writes=()):
            return dv(lambda e: e.tensor_tensor(out=out, in0=a, in1=b, op=op), reads=reads, writes=writes)

        for (dst, src) in ((LRE, lamre), (LIM, lamim), (DT, logdt)):
            k.dma("sp", dst, src[r0:r0 + 128, :], writes=[bP], acc=True)
        for (dst, src) in ((BRE, bre), (BIM, bim), (CRE, cre), (CIM, cim)):
            k.dma("sp", dst, src[r0:r0 + 128, :], writes=[bP], acc=True)
        k.dma("sp", DSK[0:32, :], dsk[l * 32:(l + 1) * 32, :], writes=[bP], acc=True)
        k.dma("sp", J1, c_j1[:, :], writes=[bP], acc=True)
        P = [bP]
        k.op("act", lambda e: e.activation(out=DT, in_=DT, func=AF.Exp), reads=P, writes=P)
        dv(lambda e: e.tensor_scalar(out=LRE, in0=LRE, scalar1=-1e-4, scalar2=None, op0=ALU.min), writes=P)
        tt_(TH, LIM, DT, ALU.mult, writes=P)
        tt_(RR, LRE, DT, ALU.mult, writes=P)
        k.op("act", lambda e: e.activation(out=RR, in_=RR, func=AF.Exp), writes=P)
        sin_reduced(SN, bP, TH, bP, TI, bP, TMPA, bP)
        sin_reduced(CSN, bP, TH, bP, TI, bP, TMPA, bP, shift=float(np.pi / 2))
        tt_(LBR, RR, CSN, ALU.mult, writes=P)
        tt_(LBI, RR, SN, ALU.mult, writes=P)
        dv(lambda e: e.tensor_scalar(out=LBR, in0=LBR, scalar1=-1.0, scalar2=None, op0=ALU.add), writes=P)
        tt_(DEN, LRE, LRE, ALU.mult, writes=P)
        tt_(TMPA, LIM, LIM, ALU.mult, writes=P)
        tt_(DEN, DEN, TMPA, ALU.add, writes=P)
        dv(lambda e: e.reciprocal(out=DEN, in_=DEN), writes=P)
        tt_(KRE, LBR, LRE, ALU.mult, writes=P)
        tt_(TMPA, LBI, LIM, ALU.mult, writes=P)
        tt_(KRE, KRE, TMPA, ALU.add, writes=P)
        tt_(KRE, KRE, DEN, ALU.mult, writes=P)
        tt_(KIM, LBI, LRE, ALU.mult, writes=P)
        tt_(TMPA, LBR, LIM, ALU.mult, writes=P)
        tt_(KIM, KIM, TMPA, ALU.subtract, writes=P)
        tt_(KIM, KIM, DEN, ALU.mult, writes=P)
        v3 = lambda a: a.rearrange("p (i h) -> p i h", h=16)
        kb = lambda a: a.unsqueeze(2).to_broadcast([128, 64, 16])
        TM3 = BPR[:, 0:1024]
        tt_(v3(BBR), v3(BRE), kb(KRE), ALU.mult, writes=P)
        tt_(v3(TM3), v3(BIM), kb(KIM), ALU.mult, writes=P)
        tt_(BBR, BBR, TM3, ALU.subtract, writes=P)
        tt_(v3(BBI), v3(BIM), kb(KRE), ALU.mult, writes=P)
        tt_(v3(TM3), v3(BRE), kb(KIM), ALU.mult, writes=P)
        tt_(BBI, BBI, TM3, ALU.add, writes=P)
        p3 = lambda a: a.rearrange("p (i c) -> p i c", c=32)
        dv(lambda e: e.memset(BPR, 0.0), writes=P)
        dv(lambda e: e.memset(BPI, 0.0), writes=P)
        dv(lambda e: e.memset(CPr, 0.0), writes=[bCP])
        dv(lambda e: e.memset(CPn, 0.0), writes=[bCP], acc=True)
        for (half, c0) in ((0, 0), (1, 16)):
            ps_ = slice(half * 64, half * 64 + 64)
            dv(lambda e, ps_=ps_, c0=c0: e.tensor_copy(out=p3(BPR)[ps_, :, c0:c0 + 16], in_=v3(BBR)[ps_, :, :]), writes=P)
            dv(lambda e, ps_=ps_, c0=c0: e.tensor_copy(out=p3(BPI)[ps_, :, c0:c0 + 16], in_=v3(BBI)[ps_, :, :]), writes=P)
            dv(lambda e, ps_=ps_, c0=c0: e.tensor_copy(out=p3(CPr)[ps_, :, c0:c0 + 16], in_=v3(CRE)[ps_, :, :]), reads=P, writes=[bCP], acc=True)
            dv(lambda e, ps_=ps_, c0=c0: e.tensor_scalar(out=p3(CPn)[ps_, :, c0:c0 + 16], in0=v3(CIM)[ps_, :, :], scalar1=-1.0, scalar2=None, op0=ALU.mult),
               reads=P, writes=[bCP], acc=True)
        for i in range(64):
            for (src, dstT) in ((BPR, LBre), (BPI, LBim)):
                pb = nxt("ps", bPS)
                k.op("pe", lambda e, src=src, i=i, pb=pb: e.transpose(pb.t[0:32, 0:128], src[:, i * 32:(i + 1) * 32], IDENT[:]),
                     reads=[bP, bIDENT], writes=[pb])
                k.op("act", lambda e, dstT=dstT, i=i, pb=pb: e.activation(out=dstT[:, i * 128:(i + 1) * 128], in_=pb.t[0:32, 0:128], func=AF.Copy),
                     reads=[pb], writes=[bLB], acc=True)
        GC = float(2.0 * np.sqrt(2.0 / np.pi))
        k.barrier()
        for i in range(64):
            dv(lambda e, i=i: e.tensor_scalar(out=TFJ, in0=J1, scalar1=TH[:, i:i + 1], scalar2=None, op0=ALU.mult), reads=[bP], writes=[bTAB])
            sin_reduced(TBS, bTAB, TFJ, bTAB, TIJ, bTAB, RT, bTAB)
            sin_reduced(TBC, bTAB, TFJ, bTAB, TIJ, bTAB, RT, bTAB, shift=float(np.pi / 2))
            dv(lambda e, i=i: e.tensor_scalar(out=RT, in0=J1, scalar1=0.0, scalar2=RR[:, i:i + 1], op0=ALU.mult, op1=ALU.add), reads=[bP], writes=[bTAB])
            for tt in range(NTT):
                t0 = tt * TT
                u32 = nxt("hc", bHC)
                ub = nxt("qb", bQB)
                U = u32.t[0:32, 0, :]
                k.dma("sp", U, sT[i * 32:(i + 1) * 32, t0:t0 + TT], reads=[k.dbuf("sT", tt)], writes=[u32])
                k.op("act", lambda e, ub=ub, U=U: e.activation(out=ub.t[:], in_=U, func=AF.Copy), reads=[u32], writes=[ub])
                pr = nxt("ps", bPS)
                mmg(pr, pr.t[:, :], [(LBre[:, i * 128:(i + 1) * 128], ub.t[:])], [bLB, ub])
                pi_ = nxt("ps", bPS)
                mmg(pi_, pi_.t[:, :], [(LBim[:, i * 128:(i + 1) * 128], ub.t[:])], [bLB, ub])
                w0, w1, w2, w3 = W4
                b0, b1, b2, b3 = bW4
                tt_(w0, pr.t[:, :], TBC, ALU.mult, reads=[pr, bTAB], writes=[b0])
                tt_(w2, pi_.t[:, :], TBS, ALU.mult, reads=[pi_, bTAB], writes=[b2])
                tt_(w0, w0, w2, ALU.add, reads=[b2], writes=[b0])
                tt_(w1, pi_.t[:, :], TBC, ALU.mult, reads=[pi_, bTAB], writes=[b1])
                tt_(w3, pr.t[:, :], TBS, ALU.mult, reads=[pr, bTAB], writes=[b3])
                tt_(w1, w1, w3, ALU.subtract, reads=[b3], writes=[b1])
                cur, prv = XS[tt % 2], XS[(tt + 1) % 2]
                bcur, bprv = bXS[tt % 2], bXS[(tt + 1) % 2]
                ini_re = 0.0 if tt == 0 else prv[0][:, 511:512]
                ini_im = 0.0 if tt == 0 else prv[1][:, 511:512]
                dv(lambda e, ini_re=ini_re: e.tensor_tensor_scan(out=w2, data0=RT, data1=w0, initial=ini_re, op0=ALU.mult, op1=ALU.add),
                   reads=[b0, bTAB, bprv], writes=[b2])
                dv(lambda e, ini_im=ini_im: e.tensor_tensor_scan(out=w3, data0=RT, data1=w1, initial=ini_im, op0=ALU.mult, op1=ALU.add),
                   reads=[b1, bTAB, bprv], writes=[b3])
                tt_(cur[0], w2, TBC, ALU.mult, reads=[b2, bTAB], writes=[bcur])
                tt_(w0, w3, TBS, ALU.mult, reads=[b3, bTAB], writes=[b0])
                tt_(cur[0], cur[0], w0, ALU.subtract, reads=[b0], writes=[bcur])
                tt_(cur[1], w2, TBS, ALU.mult, reads=[b2, bTAB], writes=[bcur])
                tt_(w1, w3, TBC, ALU.mult, reads=[b3, bTAB], writes=[b1])
                tt_(cur[1], cur[1], w1, ALU.add, reads=[b1], writes=[bcur])
                k.op("act", lambda e, cur=cur: e.activation(out=XBr, in_=cur[0], func=AF.Copy), reads=[bcur], writes=[bXB])
                k.op("act", lambda e, cur=cur: e.activation(out=XBi, in_=cur[1], func=AF.Copy), reads=[bcur], writes=[bXB], acc=True)
                py = nxt("ps", bPS)
                mmg(py, py.t[0:32, :], [(CPr[:, i * 32:(i + 1) * 32], XBr), (CPn[:, i * 32:(i + 1) * 32], XBi)], [bCP, bXB])
                y = nxt("oc", bOC)
                Y = y.t[0:32, 0, :]
                T = y.t[0:32, 1, :]
                dv(lambda e, Y=Y, U=U, i=i, py=py: e.scalar_tensor_tensor(out=Y, in0=U, scalar=DSK[0:32, i:i + 1], in1=py.t[0:32, :],
                                                                       op0=ALU.mult, op1=ALU.add), reads=[u32, py, bP], writes=[y])
                dv(lambda e, Y=Y, T=T: e.tensor_tensor(out=T, in0=Y, in1=Y, op=ALU.mult), writes=[y])
                dv(lambda e, T=T: e.tensor_scalar(out=T, in0=T, scalar1=0.044715, scalar2=1.0, op0=ALU.mult, op1=ALU.add), writes=[y])
                dv(lambda e, Y=Y, T=T: e.tensor_tensor(out=T, in0=T, in1=Y, op=ALU.mult), writes=[y])
                k.op("act", lambda e, T=T: e.activation(out=T, in_=T, func=AF.Sigmoid, scale=GC), writes=[y])
                yg = nxt("qb", bQB)
                dv(lambda e, Y=Y, T=T, yg=yg: e.tensor_tensor(out=yg.t[:], in0=Y, in1=T, op=ALU.mult), reads=[y], writes=[yg])
                k.dma("pool", ygT[i * 32:(i + 1) * 32, t0:t0 + TT], yg.t[:], reads=[yg], writes=[k.dbuf("ygT", tt)], acc=True)

    def mixout_stage(l, hsrc, sname):
        for tt in range(NTT):
            t0 = tt * TT
            k.dma("sp", X16[:, 0:16, :], oT.rearrange("(c p) t -> p c t", p=128)[:, :, t0:t0 + TT], reads=[k.dbuf("oT", tt)], writes=[bX16])
            k.dma("sp", X16[:, 16:32, :], ygT.rearrange("(c p) t -> p c t", p=128)[:, :, t0:t0 + TT], reads=[k.dbuf("ygT", tt)], writes=[bX16], acc=True)
            st = {}

            def gate_load(src, nm, j):
                hc = nxt("hc", bHC)
                k.dma("sp", hc.t[:], src[j * 256:(j + 1) * 256, t0:t0 + TT].rearrange("(c p) t -> p c t", p=128),
                      reads=[k.dbuf(nm, tt)], writes=[hc])
                return hc

            def epi_a(j, half, pb):
                if half == 0:
                    st["hc"] = gate_load(gaT, "gaT", j)
                hc = st["hc"]
                oc = 2 * j + half
                k.op("dve", lambda e: e.tensor_tensor(out=A32[:, oc, :], in0=pb.t[:], in1=hc.t[:, half, :], op=ALU.mult),
                     reads=[pb, hc], writes=[bA32], acc=True)
            gemm(X16[:, 0:16, :], bX16, 16, "w_attn_out", l, 16, range(16), epi_a)

            def epi_g(j, half, pb):
                c = 2 * j + half
                t = bT32[2 + (c % 2)]
                k.op("act", lambda e: e.activation(out=t.t[:], in_=pb.t[:], func=AF.Sigmoid), reads=[pb], writes=[t])
                k.op("dve", lambda e: e.tensor_tensor(out=H16[:, c, :], in0=t.t[:], in1=X16[:, 16 + c, :], op=ALU.mult),
                     reads=[t, bX16], writes=[bH16], acc=True)
            gemm(X16[:, 16:32, :], bX16, 16, "w_glu", l, 8, range(8), epi_g)

            def epi_s(j, half, pb):
                if half == 0:
                    st["hc"] = gate_load(gsT, "gsT", j)
                hc = st["hc"]
                oc = 2 * j + half
                t = bT32[2 + (oc % 2)]
                k.op("dve", lambda e: e.tensor_tensor(out=t.t[:], in0=pb.t[:], in1=hc.t[:, half, :], op=ALU.mult), reads=[pb, hc], writes=[t])
                k.op("dve", lambda e: e.tensor_tensor(out=X16[:, oc, :], in0=t.t[:], in1=A32[:, oc, :], op=ALU.add),
                     reads=[t, bA32], writes=[bX16], acc=True)
            gemm(H16[:, 0:16, :], bH16, 16, "w_ssm_out", l, 16, range(16), epi_s)
            gemm(X16, bX16, 32, "w_o", l, 16, range(16), copy_to_A32)
            post_res(l, 3 * 32, 1.0, hsrc, sname, hT, "hT", tt)

    def xa_stage(l, hsrc, sname):
        norm_in(memT, "memT", 0, MG, 0, ncol=NMEM)

        def epi_kv(j, half, pb):
            c = 2 * j + half
            if c < 4:
                k.op("act", lambda e: e.activation(out=KM[:, c, :], in_=pb.t[:, 0:NMEM], func=AF.Copy), reads=[pb], writes=[bKM], acc=True)
            else:
                hd = c - 4
                vt = nxt("q16", bQ16)
                k.op("act", lambda e: e.activation(out=vt.t[:, 0:NMEM], in_=pb.t[:, 0:NMEM], func=AF.Copy), reads=[pb], writes=[vt])
                for hf in range(2):
                    pt = nxt("psb", bPSB)
                    k.op("pe", lambda e, hf=hf, pt=pt: e.transpose(pt.t[:, 0:128], vt.t[:, hf * 128:(hf + 1) * 128], IDB[:]),
                         reads=[vt, bIDB], writes=[pt])
                    k.op("act", lambda e, hf=hf, pt=pt: e.activation(out=VM[:, hd * 2 + hf, :], in_=pt.t[:, 0:128], func=AF.Copy),
                         reads=[pt], writes=[bVM], acc=True)
        gemm(X16, bX16, 32, "xa_wkv", l, 4, range(4), epi_kv, ncol=NMEM)
        for tt in range(NTT):
            norm_in(hsrc, sname, tt, GN, 4 * 32)

            def epi_q(j, half, pb):
                hd = 2 * j + half
                k.op("act", lambda e: e.activation(out=H16[:, hd, :], in_=pb.t[:], func=AF.Copy), reads=[pb], writes=[bH16], acc=True)
            gemm(X16, bX16, 32, "xa_wq", l, 2, range(2), epi_q)
            for b in range(4):
                for hd in range(4):
                    attn_unit(H16[:, hd, b * 128:(b + 1) * 128], bH16, KM[:, hd, :], bKM, VM[:, hd * 2, :], VM[:, hd * 2 + 1, :], bVM,
                              None, None, None, H16[:, 4 + hd, b * 128:(b + 1) * 128], bH16)
            gemm(H16[:, 4:8, :], bH16, 4, "xa_wo", l, 16, range(16), copy_to_A32)
            post_res(l, 5 * 32, 1.0, hsrc, sname, hT, "hT", tt)
            bH16.w, bH16.r = [], list(bH16.r)
    STAGES = stop_after if stop_after is not None else ("ffn1", "proj", "attn", "ssm", "mix", "xa", "ffn2")
    if "proj" in STAGES:
        rope_tables()
    k.barrier()
    cur, cname = xT, "xT"
    for l in range(depth):
        load_gains(l)
        if "ffn1" in STAGES:
            ffn_stage(l, "ffn1", 0, 1, cur, cname, hT, "hT")
            cur, cname = hT, "hT"
            k.barrier()
        if "proj" in STAGES:
            mixer_proj(l, cur, cname)
            k.barrier()
        if "attn" in STAGES:
            attention_stage(l)
            k.barrier()
        if "ssm" in STAGES:
            ssm_stage(l)
            k.barrier()
        if "mix" in STAGES:
            mixout_stage(l, cur, cname)
            cur, cname = hT, "hT"
            k.barrier()
        if "xa" in STAGES:
            xa_stage(l, cur, cname)
            cur, cname = hT, "hT"
            k.barrier()
        if "ffn2" in STAGES:
            last = (l == depth - 1)
            ffn_stage(l, "ffn2", 6, 7, cur, cname, yT if last else hT, "yT" if last else "hT")
            cur, cname = hT, "hT"
            k.barrier()
    k.barrier()

    with nc.Block() as block:
        @block.sync
        def _(e):
            k.replay("sp", e)

        @block.scalar
        def _(e):
            k.replay("act", e)

        @block.vector
        def _(e):
            k.replay("dve", e)

        @block.tensor
        def _(e):
            k.replay("pe", e)

        @block.gpsimd
        def _(e):
            k.replay("pool", e)
    es.close()
    return nc


def prep_inputs(inp, NT, depth):
    f = np.float32
    m = {}
    m["xT"] = np.ascontiguousarray(np.asarray(inp["x"])[0, :NT].T.astype(f))
    m["memT"] = np.ascontiguousarray(np.asarray(inp["mem"])[0].T.astype(f))
    m["pos"] = np.ascontiguousarray(np.asarray(inp["positions"])[:, :NT].astype(np.int32))
    g = np.asarray(inp["norm_gains"])[:depth].reshape(depth, 8, 32, 128)
    m["gains"] = np.ascontiguousarray(g.transpose(0, 3, 1, 2)).reshape(depth * 128, 256)
    mg = np.asarray(inp["mem_norm_gain"])[:depth].reshape(depth, 32, 128)
    m["mgain"] = np.ascontiguousarray(mg.transpose(0, 2, 1)).reshape(depth * 128, 32)
    for (n, K, OUT) in WSPEC:
        w = np.asarray(inp[n])
        m[n] = np.concatenate([tile_w(w[l], pair_gu=("gu" in n)) for l in range(depth)], axis=0)
    m["sinks"] = np.ascontiguousarray(np.broadcast_to(np.asarray(inp["attn_sinks"])[:depth, None, :], (depth, 128, 16))).reshape(depth * 128, 16)

    def ps_layout(a):
        a = np.asarray(a)[:depth].reshape(depth, 64, 2, 64)
        return np.ascontiguousarray(a.transpose(0, 2, 3, 1)).reshape(depth * 128, 64)
    m["lamre"] = ps_layout(inp["ssm_lambda_re"])
    m["lamim"] = ps_layout(inp["ssm_lambda_im"])
    m["logdt"] = ps_layout(np.broadcast_to(np.asarray(inp["ssm_log_dt"])[:, :, None], (DEPTH, 128, 64)))
    b = np.asarray(inp["ssm_b_re"])[:depth].reshape(depth, 64, 2, 64, 16)
    m["bre"] = np.ascontiguousarray(b.transpose(0, 2, 3, 1, 4)).reshape(depth * 128, 64 * 16)
    b = np.asarray(inp["ssm_b_im"])[:depth].reshape(depth, 64, 2, 64, 16)
    m["bim"] = np.ascontiguousarray(b.transpose(0, 2, 3, 1, 4)).reshape(depth * 128, 64 * 16)
    c = np.asarray(inp["ssm_c_re"])[:depth].reshape(depth, 64, 2, 16, 64)
    m["cre"] = np.ascontiguousarray(c.transpose(0, 2, 4, 1, 3)).reshape(depth * 128, 64 * 16)
    c = np.asarray(inp["ssm_c_im"])[:depth].reshape(depth, 64, 2, 16, 64)
    m["cim"] = np.ascontiguousarray(c.transpose(0, 2, 4, 1, 3)).reshape(depth * 128, 64 * 16)
    d = np.asarray(inp["ssm_d"])[:depth].reshape(depth, 64, 32)
    m["dsk"] = np.ascontiguousarray(d.transpose(0, 2, 1)).reshape(depth * 32, 64)
    m["c_ident"] = np.eye(128, dtype=f)
    rot = np.zeros((32, 32), f)
    for i in range(16):
        rot[i + 16, i] = -1.0
        rot[i, i + 16] = 1.0
    m["c_rotT"] = rot
    invf = (500000.0 ** (-np.arange(0, 32, 2, dtype=np.float32) / 32.0)).astype(f)
    m["c_invf"] = np.concatenate([invf, invf]).reshape(32, 1)
    m["c_j1"] = np.ascontiguousarray(np.broadcast_to(np.arange(1, 513, dtype=f)[None, :], (128, 512)))
    qc = np.arange(128) // 64
    kc = np.arange(256) // 64
    band = (kc[None, :] >= qc[:, None]) & (kc[None, :] <= qc[:, None] + 2)
    mk = np.where(band, 0.0, -1e30).astype(f)
    mk0 = mk.copy()
    mk0[:, :128] = -1e30
    m["c_mask"] = np.concatenate([mk, mk0], axis=1)
    return {kk: np.ascontiguousarray(v) for kk, v in m.items()}


PER_LAYER = ("norm_gains", "mem_norm_gain", "attn_sinks", "ssm_lambda_re", "ssm_lambda_im", "ssm_log_dt",
             "ssm_b_re", "ssm_b_im", "ssm_c_re", "ssm_c_im", "ssm_d") + tuple(n for (n, _, _) in WSPEC)
LAYERS_PER_LAUNCH = 1


def kernel(**inputs):
    NT = NT_FULL
    lpl = LAYERS_PER_LAUNCH
    nc = build(NT, lpl)
    hT = None
    for l0 in range(0, DEPTH, lpl):
        sub = {kk: (np.asarray(v)[l0:l0 + lpl] if kk in PER_LAYER else v) for kk, v in inputs.items()}
        m = prep_inputs(sub, NT, lpl)
        if hT is not None:
            m["xT"] = hT
        res = run_bass_kernel_spmd(nc, [m], core_ids=[0])
        hT = np.ascontiguousarray(res.results[0]["yT"])
    return np.ascontiguousarray(hT.T)[None].astype(np.float32)
```
